# Optimizing a Trainium2 kernel written in Bass

```python
import math
import jax, jax.numpy as jnp
from jax import lax
import numpy as np

D_MODEL = 1024
BATCH = 4
SEQ = 4096
DEPTH = 2

HEAD_DIM = 64
N_ATTN_HEADS = 8
ATTN_WIDTH = N_ATTN_HEADS * HEAD_DIM
LRU_WIDTH = D_MODEL - ATTN_WIDTH
N_LRU_BLOCKS = 8
LRU_BLOCK = LRU_WIDTH // N_LRU_BLOCKS
MIX_WIDTH = ATTN_WIDTH + LRU_WIDTH
IN_WIDTH = 3 * ATTN_WIDTH + 2 * LRU_WIDTH
CONV_WIDTH = 4
LRU_C = 8.0
D_FF = 4 * D_MODEL
BLOCK_Q = 128
EPS = 1e-6

kernel_name = 'hymba_style_rglru_stickbreaking_block'


def rms_norm(x, g):
    xf = x.astype(jnp.float32)
    y = xf * lax.rsqrt(jnp.mean(xf * xf, axis=-1, keepdims=True) + EPS)
    return (y * g.astype(jnp.float32)).astype(x.dtype)


def causal_depthwise_conv(u, w, b):
    S = u.shape[1]
    up = jnp.pad(u, ((0, 0), (CONV_WIDTH - 1, 0), (0, 0)))
    out = b
    for k in range(CONV_WIDTH):
        out = out + up[:, k:k + S] * w[k]
    return out


def _linear_rec_combine(left, right):
    a1, b1 = left
    a2, b2 = right
    return a1 * a2, a2 * b1 + b2


def rg_lru(u, w_rg, b_rg, w_ig, b_ig, lam):
    B, S, _ = u.shape
    ub = u.reshape(B, S, N_LRU_BLOCKS, LRU_BLOCK)
    r = jax.nn.sigmoid(jnp.einsum('bsnc,ncd->bsnd', ub, w_rg) + b_rg.reshape(N_LRU_BLOCKS, LRU_BLOCK))
    i = jax.nn.sigmoid(jnp.einsum('bsnc,ncd->bsnd', ub, w_ig) + b_ig.reshape(N_LRU_BLOCKS, LRU_BLOCK))
    r = r.reshape(B, S, LRU_WIDTH).astype(jnp.float32)
    i = i.reshape(B, S, LRU_WIDTH).astype(jnp.float32)
    log_a = -LRU_C * r * jax.nn.softplus(-lam.astype(jnp.float32))
    a = jnp.exp(log_a)
    mult = jnp.sqrt(-jnp.expm1(2.0 * log_a))
    bterm = mult * (i * u.astype(jnp.float32))
    _, h = lax.associative_scan(_linear_rec_combine, (a, bterm), axis=1)
    return h.astype(u.dtype)


def stick_breaking_attention(q, k, v):
    S = q.shape[1]
    scale = 1.0 / math.sqrt(HEAD_DIM)
    qf = q.astype(jnp.float32)
    kf = k.astype(jnp.float32)
    vf = v.astype(jnp.float32)
    outs = []
    for blk in range(S // BLOCK_Q):
        q0 = blk * BLOCK_Q
        end = q0 + BLOCK_Q
        z = jnp.einsum('bqhd,bkhd->bhqk', qf[:, q0:end], kf[:, :end]) * scale
        qpos = q0 + jnp.arange(BLOCK_Q)[:, None]
        kpos = jnp.arange(end)[None, :]
        mask = kpos < qpos
        log_one_minus = jnp.where(mask, jax.nn.log_sigmoid(-z), 0.0)
        suffix = lax.cumsum(log_one_minus, axis=3, reverse=True) - log_one_minus
        log_w = jax.nn.log_sigmoid(z) + suffix
        w = jnp.where(mask, jnp.exp(log_w), 0.0)
        outs.append(jnp.einsum('bhqk,bkhd->bqhd', w, vf[:, :end]))
    return jnp.concatenate(outs, axis=1).astype(v.dtype)


def hybrid_mixer(h, w_in, conv_w, conv_b, w_rg, b_rg, w_ig, b_ig, lam, q_norm_g, k_norm_g, w_out):
    B, S, _ = h.shape
    proj = h @ w_in
    q, k, v, xl, gl = jnp.split(proj, [ATTN_WIDTH, 2 * ATTN_WIDTH, 3 * ATTN_WIDTH, 3 * ATTN_WIDTH + LRU_WIDTH], axis=-1)
    q = rms_norm(q.reshape(B, S, N_ATTN_HEADS, HEAD_DIM), q_norm_g)
    k = rms_norm(k.reshape(B, S, N_ATTN_HEADS, HEAD_DIM), k_norm_g)
    v = v.reshape(B, S, N_ATTN_HEADS, HEAD_DIM)
    attn = stick_breaking_attention(q, k, v).reshape(B, S, ATTN_WIDTH)
    xl = causal_depthwise_conv(xl, conv_w, conv_b)
    lru = rg_lru(xl, w_rg, b_rg, w_ig, b_ig, lam) * jax.nn.gelu(gl)
    return jnp.concatenate([attn, lru], axis=-1) @ w_out


def sq_relu_mlp(h, w_up, w_down):
    return jnp.square(jax.nn.relu(h @ w_up)) @ w_down


def setup_inputs(seed: int = 0) -> dict:
    key = jax.random.key(seed)
    ks = jax.random.split(key, 17)
    f32 = jnp.float32
    nrm = lambda kk, shape, s: jax.random.normal(kk, shape, f32) * s
    a0 = jax.random.uniform(ks[10], (DEPTH, LRU_WIDTH), f32, minval=0.9, maxval=0.999)
    return {
        'x': jax.random.normal(ks[0], (BATCH, SEQ, D_MODEL), f32),
        'norm1_g': 1.0 + nrm(ks[1], (DEPTH, D_MODEL), 0.02),
        'w_in': nrm(ks[2], (DEPTH, D_MODEL, IN_WIDTH), D_MODEL ** -0.5),
        'conv_w': nrm(ks[3], (DEPTH, CONV_WIDTH, LRU_WIDTH), CONV_WIDTH ** -0.5),
        'conv_b': nrm(ks[4], (DEPTH, LRU_WIDTH), 0.01),
        'w_rg': nrm(ks[5], (DEPTH, N_LRU_BLOCKS, LRU_BLOCK, LRU_BLOCK), LRU_BLOCK ** -0.5),
        'b_rg': nrm(ks[6], (DEPTH, LRU_WIDTH), 0.01),
        'w_ig': nrm(ks[7], (DEPTH, N_LRU_BLOCKS, LRU_BLOCK, LRU_BLOCK), LRU_BLOCK ** -0.5),
        'b_ig': nrm(ks[8], (DEPTH, LRU_WIDTH), 0.01),
        'lru_lambda': jnp.log(a0) - jnp.log1p(-a0),
        'q_norm_g': 1.0 + nrm(ks[11], (DEPTH, HEAD_DIM), 0.02),
        'k_norm_g': 1.0 + nrm(ks[12], (DEPTH, HEAD_DIM), 0.02),
        'w_out': nrm(ks[13], (DEPTH, MIX_WIDTH, D_MODEL), MIX_WIDTH ** -0.5),
        'norm2_g': 1.0 + nrm(ks[14], (DEPTH, D_MODEL), 0.02),
        'w_up': nrm(ks[15], (DEPTH, D_MODEL, D_FF), D_MODEL ** -0.5),
        'w_down': nrm(ks[16], (DEPTH, D_FF, D_MODEL), D_FF ** -0.5),
    }


def reference(x, norm1_g, w_in, conv_w, conv_b, w_rg, b_rg, w_ig, b_ig, lru_lambda, q_norm_g, k_norm_g, w_out, norm2_g, w_up, w_down):
    for l in range(DEPTH):
        h = rms_norm(x, norm1_g[l])
        x = x + hybrid_mixer(h, w_in[l], conv_w[l], conv_b[l], w_rg[l], b_rg[l], w_ig[l], b_ig[l],
                             lru_lambda[l], q_norm_g[l], k_norm_g[l], w_out[l])
        h = rms_norm(x, norm2_g[l])
        x = x + sq_relu_mlp(h, w_up[l], w_down[l])
    return x
```

```python
import os
import numpy as np
import ml_dtypes
from contextlib import ExitStack
import concourse.bass as bass
import concourse.mybir as mybir
from concourse.bass_utils import run_bass_kernel_spmd

F32 = mybir.dt.float32
BF16 = mybir.dt.bfloat16
AF = mybir.ActivationFunctionType
ALU = mybir.AluOpType

D = 1024
S_LEN = 4096
NB = 4
DEPTH = 2
EPS = 1e-6
ENGS = ["tensor", "scalar", "vector", "gpsimd", "sync"]


class Sched:
    def __init__(self, nc, es):
        self.nc = nc
        self.es = es
        self.sems = {}
        self.cnt = {}
        self.ops = {e: [] for e in ENGS}
        self.waited = {e: {} for e in ENGS}
        self.bw = {}
        self.br = {}

    def sem(self, name):
        if name not in self.sems:
            self.sems[name] = self.es.enter_context(self.nc.semaphore(name))
            self.cnt[name] = 0
        return self.sems[name]

    def add(self, eng, fn, reads=(), writes=(), dma=None):
        deps = {}

        def merge(d):
            for s, v in d.items():
                if deps.get(s, 0) < v:
                    deps[s] = v

        for b in reads:
            merge(self.bw.get(b, {}))
        for b in writes:
            merge(self.bw.get(b, {}))
            merge(self.br.get(b, {}))
        if dma is None:
            sname, inc = "e_" + eng, 1
        else:
            sname, inc = "d_" + dma, 16
        self.sem(sname)
        if eng == "tensor" and dma is None:
            deps.pop("e_tensor", None)
        waits = []
        w = self.waited[eng]
        for s, v in deps.items():
            if w.get(s, 0) < v:
                waits.append((s, v))
                w[s] = v
        self.cnt[sname] += inc
        tok = (sname, self.cnt[sname])
        self.ops[eng].append((waits, fn, sname, inc))
        for b in writes:
            self.bw[b] = {tok[0]: tok[1]}
            self.br[b] = {}
        for b in reads:
            d = self.br.setdefault(b, {})
            if d.get(tok[0], 0) < tok[1]:
                d[tok[0]] = tok[1]
        return tok

    def barrier(self, engines=ENGS):
        cur = {s: v for s, v in self.cnt.items() if v > 0}
        for e in engines:
            waits = []
            for s, v in cur.items():
                if self.waited[e].get(s, 0) < v:
                    waits.append((s, v))
                    self.waited[e][s] = v
            if waits:
                self.ops[e].append((waits, None, None, 0))

    def flush(self):
        nc = self.nc
        sems = self.sems
        with nc.Block() as blk:
            for e in ENGS:
                ops = self.ops[e]
                if not ops:
                    continue

                def body(eng, ops=ops):
                    for waits, fn, sname, inc in ops:
                        for s, v in waits:
                            eng.wait_ge(sems[s], v)
                        if fn is not None:
                            fn(eng).then_inc(sems[sname], inc)

                getattr(blk, e)(body)
        self.ops = {e: [] for e in ENGS}


def MM(S, out, lhsT, rhs, start, stop, reads, writes, skip=False):
    return S.add("tensor", lambda e: e.matmul(out, lhsT=lhsT, rhs=rhs, start=start, stop=stop,
                                              skip_group_check=skip), reads, writes)


def ACT(S, out, in_, func, reads, writes, bias=None, scale=None):
    kw = {}
    if bias is not None:
        kw["bias"] = bias
    if scale is not None:
        kw["scale"] = scale
    return S.add("scalar", lambda e: e.activation(out=out, in_=in_, func=func, **kw), reads, writes)


def TT(S, eng, out, in0, in1, op, reads, writes):
    return S.add(eng, lambda e: e.tensor_tensor(out=out, in0=in0, in1=in1, op=op), reads, writes)


def TS(S, eng, out, in0, s1, s2, op0, op1, reads, writes):
    if op1 is None:
        return S.add(eng, lambda e: e.tensor_scalar(out=out, in0=in0, scalar1=s1, scalar2=None, op0=op0),
                     reads, writes)
    return S.add(eng, lambda e: e.tensor_scalar(out=out, in0=in0, scalar1=s1, scalar2=s2, op0=op0, op1=op1),
                 reads, writes)


def STT(S, out, in0, scalar, in1, op0, op1, reads, writes):
    return S.add("vector", lambda e: e.scalar_tensor_tensor(out=out, in0=in0, scalar=scalar, in1=in1,
                                                            op0=op0, op1=op1), reads, writes)


def CP(S, eng, out, in_, reads, writes):
    return S.add(eng, lambda e: e.tensor_copy(out=out, in_=in_), reads, writes)


def RCP(S, out, in_, reads, writes):
    return S.add("vector", lambda e: e.reciprocal(out=out, in_=in_), reads, writes)


def MEMSET(S, eng, out, val, reads, writes):
    return S.add(eng, lambda e: e.memset(out, val), reads, writes)


def DMA(S, eng, out, in_, stream, reads, writes):
    return S.add(eng, lambda e: e.dma_start(out=out, in_=in_), reads, writes, dma=stream)


def SCAN(S, out, d0, d1, init, reads, writes):
    return S.add("vector", lambda e: e.tensor_tensor_scan(out=out, data0=d0, data1=d1, initial=init,
                                                          op0=ALU.mult, op1=ALU.add), reads, writes)


class Ctx:
    pass


_UID = [0]


def _uname(n):
    _UID[0] += 1
    return "s%d_%s" % (_UID[0], n)


def setup_common(nc, es, S, cst_dram):
    C = Ctx()
    C.nc, C.es, C.S = nc, es, S
    sb = lambda n, sh, dt: es.enter_context(nc.sbuf_tensor(_uname(n), sh, dt))
    C.ps = [es.enter_context(nc.psum_tensor("ps%d" % i, [128, 512], F32)) for i in range(8)]
    C.cf = sb("cst_f", [128, 5, 128], F32)
    C.cb = sb("cst_b", [128, 2, 128], BF16)
    C.kc = sb("kconst", [128, 4], F32)
    DMA(S, "sync", C.cf[:], cst_dram, "cst", [], ["cst_f"])
    DMA(S, "gpsimd", C.cb[:], cst_dram[:, 0:2, :], "cstb", [], ["cst_b"])
    MEMSET(S, "vector", C.kc[:, 0:1], 1.0, [], ["kc0"])
    MEMSET(S, "vector", C.kc[:, 1:2], EPS, [], ["kc1"])
    MEMSET(S, "vector", C.kc[:, 2:3], 0.0, [], ["kc2"])
    return C


def emit_norm(C, x_sb, xid, t0, gcols, gid, dst_fn, dst_ids, tmp, psb):
    S = C.S
    ones = C.cf[:, 3, :]
    bank = C.ps[psb]
    bid = ("ps", psb)
    for k in range(8):
        sq = tmp.sq[k % 3]
        sqid = ("nsq", k % 3)
        TT(S, "gpsimd", sq[:], x_sb[:, k, t0:t0 + 512], x_sb[:, k, t0:t0 + 512], ALU.mult, [xid(k)], [sqid])
        MM(S, bank[:], ones, sq[:], k == 0, k == 7, ["cst_f", sqid], [bid])
    r = tmp.rstd[0]
    rid = ("nrstd", 0)
    ACT(S, r[:], bank[:], AF.Ln, [bid, "kc1"], [rid], bias=C.kc[:, 1:2], scale=1.0 / D)
    ACT(S, r[:], r[:], AF.Exp, [rid], [rid], scale=-0.5)
    for m in range(8):
        STT(S, dst_fn(m), x_sb[:, m, t0:t0 + 512], gcols[:, m:m + 1], r[:], ALU.mult, ALU.mult,
            [xid(m), gid, rid], [dst_ids(m)])


def emit_P0(C, x_dram, pv2_dram, hT_out, NT):
    nc, es, S = C.nc, C.es, C.S
    with ExitStack() as es2:
        sb = lambda n, sh, dt: es2.enter_context(nc.sbuf_tensor(_uname(n), sh, dt))
        x_sb = sb("x_sb", [128, 8, NT], F32)
        pv2 = sb("pv2", [128, 16], F32)
        tmp = Ctx()
        tmp.sq = [sb("nsq%d" % i, [128, 512], F32) for i in range(3)]
        tmp.rstd = [sb("nrstd0", [128, 512], F32)]
        hst = [sb("hst%d" % i, [128, 8, 512], BF16) for i in range(2)]
        DMA(S, "sync", pv2[:], pv2_dram, "pv2", [], ["pv2"])
        ntc = NT // 512
        for tc in range(ntc):
            DMA(S, "sync", x_sb[:, :, tc * 512:(tc + 1) * 512],
                x_dram[:, :, tc * 512:(tc + 1) * 512].rearrange("c p t -> p c t"), "x%d" % (tc % 4),
                [], [("x", tc, k) for k in range(8)])
        for tc in range(ntc):
            h = hst[tc % 2]
            emit_norm(C, x_sb, lambda k, tc=tc: ("x", tc, k), tc * 512, pv2[:, 8:16], "pv2",
                      lambda m, h=h: h[:, m, :], lambda m, tc=tc: ("hst", tc % 2, m), tmp, 0)
            DMA(S, "sync", hT_out[:, :, tc * 512:(tc + 1) * 512].rearrange("c p t -> p c t"), h[:],
                "hst%d" % (tc % 2), [("hst", tc % 2, m) for m in range(8)], [("hT_dram", tc)])
        S.barrier()
        S.flush()


def emit_P1(C, hT_dram, w_in_dram, pv_dram, wg_dram, mix_out):
    nc, S = C.nc, C.S
    NT = S_LEN
    NTC = NT // 512
    with ExitStack() as es2:
        sb = lambda n, sh, dt: es2.enter_context(nc.sbuf_tensor(_uname(n), sh, dt))
        ps = C.ps
        w_in = sb("w_in", [128, 8, 1280], BF16)
        wg = sb("wg", [128, 2, 2, 128], BF16)
        pv = sb("pv", [128, 24], F32)
        hT = [sb("hT%d" % i, [128, 8, 512], BF16) for i in range(2)]
        qT = sb("qT", [128, 2, NT], BF16)
        kT = sb("kT", [128, 2, NT], BF16)
        V = sb("V", [128, 32, 256], BF16)
        qraw = [sb("qraw%d" % i, [128, 512], F32) for i in range(2)]
        qsq = [sb("qsq%d" % i, [128, 512], F32) for i in range(2)]
        qrs = [sb("qrs%d" % i, [128, 512], F32) for i in range(2)]
        xle = [sb("xle%d" % c, [128, 515], F32) for c in range(2)]
        gl = [sb("gl%d" % c, [128, 512], F32) for c in range(2)]
        u = [sb("u%d" % c, [128, 512], F32) for c in range(2)]
        ub = [sb("ub%d" % c, [128, 512], BF16) for c in range(2)]
        rr = [sb("rr%d" % c, [128, 512], F32) for c in range(2)]
        ii = [sb("ii%d" % c, [128, 512], F32) for c in range(2)]
        aa = [sb("aa%d" % c, [128, 512], F32) for c in range(2)]
        bb = [sb("bb%d" % c, [128, 512], F32) for c in range(2)]
        hs = [[sb("hs%d_%d" % (c, j), [128, 512], F32) for j in range(2)] for c in range(2)]
        gt = [sb("gt%d" % c, [128, 512], F32) for c in range(2)]
        gu = [sb("gu%d" % c, [128, 512], F32) for c in range(2)]
        lo = [sb("lo%d" % c, [128, 512], BF16) for c in range(2)]
        eb = [[sb("e%d_%d" % (h, j), [128, 512], F32) for j in range(2)] for h in range(2)]
        spb = [[sb("sp%d_%d" % (h, j), [128, 512], BF16) for j in range(2)] for h in range(2)]
        gb = [[sb("g%d_%d" % (h, j), [128, 512], F32) for j in range(2)] for h in range(2)]
        wb = [[sb("w%d_%d" % (h, j), [128, 512], BF16) for j in range(2)] for h in range(2)]
        ost = [sb("ost%d" % j, [128, 512], BF16) for j in range(2)]

        DMA(S, "gpsimd", w_in[:], w_in_dram.rearrange("(c p) n -> p c n", p=128), "w_in", [], ["w_in"])
        DMA(S, "sync", pv[:, 0:18], pv_dram, "pv", [], ["pv"])
        MEMSET(S, "gpsimd", wg[:], 0.0, [], [("wg", i) for i in range(8)])
        for g in (range(2) if not os.environ.get('SKIP_WG') else []):
            for blk in range(4):
                c, half = blk // 2, blk % 2
                DMA(S, "gpsimd", wg[half * 64:(half + 1) * 64, g, c, half * 64:(half + 1) * 64],
                    wg_dram[g, blk, :, :], "wg", [], [("wg", g * 4 + blk)])
        wgids = [("wg", i) for i in range(8)]
        TS(S, "vector", pv[:, 18:22], pv[:, 10:14], -1.0, None, ALU.mult, None, ["pv"], ["pvd"])
        ACT(S, pv[:, 22:24], pv[:, 14:16], AF.Exp, ["pv"], ["pvs"], scale=-1.0)
        ACT(S, pv[:, 22:24], pv[:, 22:24], AF.Ln, ["pvs", "kc0"], ["pvs"], bias=C.kc[:, 0:1])
        TS(S, "vector", pv[:, 22:24], pv[:, 22:24], -8.0, None, ALU.mult, None, ["pvs"], ["pvs"])
        TS(S, "vector", pv[:, 16:17], pv[:, 16:17], 0.125, None, ALU.mult, None, ["pv"], ["pvq"])
        for c in range(2):
            MEMSET(S, "vector", xle[c][:, 0:3], 0.0, [], [("xle", c)])

        def load_h(tc):
            DMA(S, "sync", hT[tc % 2][:], hT_dram[:, :, tc * 512:(tc + 1) * 512].rearrange("c p t -> p c t"),
                "hT%d" % (tc % 2), [], [("hT", tc % 2)])

        blockones = C.cf[:, 4, :]
        pcount = [0]

        def next_bank():
            b = pcount[0] % 6
            pcount[0] += 1
            return b

        load_h(0)
        NTCL = int(os.environ.get('P1_NTC', NTC))
        for tc in range(NTCL):
            if tc + 1 < NTCL:
                load_h(tc + 1)
            hsl = hT[tc % 2]
            hid = ("hT", tc % 2)
            t0 = tc * 512

            def proj(col0, bank):
                for k in range(8):
                    MM(S, ps[bank][:], w_in[:, k, col0:col0 + 128], hsl[:, k, :], k == 0, k == 7,
                       ["w_in", hid], [("ps", bank)])

            for j in (range(int(os.environ.get('QK_J', 4))) if not os.environ.get('SKIP_QK') else []):
                isq = j < 2
                c = j % 2
                col0 = (0 if isq else 256) + c * 128
                bank = next_bank()
                proj(col0, bank)
                s = j % 2
                CP(S, "vector", qraw[s][:], ps[bank][:], [("ps", bank)], [("qraw", s)])
                qkm = os.environ.get('QK_MODE', '')
                if qkm == 'onlymm':
                    continue
                TT(S, "gpsimd", qsq[s][:], qraw[s][:], qraw[s][:], ALU.mult, [("qraw", s)], [("qsq", s)])
                if qkm == 'nomm':
                    CP(S, "vector", qrs[s][:], qsq[s][:], [("qsq", s)], [("qrs", s)])
                else:
                    b2 = next_bank()
                    MM(S, ps[b2][:], blockones, qsq[s][:], True, True, ["cst_f", ("qsq", s)], [("ps", b2)])
                    ACT(S, qrs[s][:], ps[b2][:], AF.Ln, [("ps", b2), "kc1"], [("qrs", s)],
                        bias=C.kc[:, 1:2], scale=1.0 / 64)
                    ACT(S, qrs[s][:], qrs[s][:], AF.Exp, [("qrs", s)], [("qrs", s)], scale=-0.5)
                dst = (qT if isq else kT)[:, c, t0:t0 + 512]
                gcol = pv[:, 16:17] if isq else pv[:, 17:18]
                if qkm != 'nostt':
                    STT(S, dst, qraw[s][:], gcol, qrs[s][:], ALU.mult, ALU.mult,
                        [("qraw", s), ("qrs", s), "pv", "pvq"], [("qk", isq, c, tc)])
            for half in (range(2) if not os.environ.get('SKIP_V') else []):
                bank = next_bank()
                for tb2 in range(2):
                    tb = half * 2 + tb2
                    for k in range(8):
                        MM(S, ps[bank][:, tb2 * 256:(tb2 + 1) * 256], hsl[:, k, tb * 128:(tb + 1) * 128],
                           w_in[:, k, 512:768], k == 0, k == 7, ["w_in", hid], [("ps", bank)])
                CP(S, "scalar" if False else "vector",
                   V[:, tc * 4 + half * 2:tc * 4 + half * 2 + 2, :],
                   ps[bank][:].rearrange("p (a b) -> p a b", a=2), [("ps", bank)], [("V", tc, half)])
            for c in (range(2) if "lru" in _P1_PARTS else []):
                bx = next_bank()
                proj(768 + c * 128, bx)
                CP(S, "vector", xle[c][:, 3:515], ps[bx][:], [("ps", bx)], [("xle", c)])
                bg = next_bank()
                proj(1024 + c * 128, bg)
                CP(S, "vector", gl[c][:], ps[bg][:], [("ps", bg)], [("gl", c)])
                cw = lambda k: pv[:, c * 4 + k:c * 4 + k + 1]
                TS(S, "vector", u[c][:], xle[c][:, 3:515], cw(3), pv[:, 8 + c:9 + c], ALU.mult, ALU.add,
                   [("xle", c), "pv"], [("u", c)])
                for k in (2, 1, 0):
                    STT(S, u[c][:], xle[c][:, k:k + 512], cw(k), u[c][:], ALU.mult, ALU.add,
                        [("xle", c), "pv", ("u", c)], [("u", c)])
                CP(S, "gpsimd", ub[c][:], u[c][:], [("u", c)], [("ub", c)])
                CP(S, "gpsimd", xle[c][:, 0:3], xle[c][:, 512:515], [("xle", c)], [("xle", c)])
                br_ = next_bank()
                MM(S, ps[br_][:], wg[:, 0, c, :], ub[c][:], True, True, wgids + [("ub", c)], [("ps", br_)])
                bi_ = next_bank()
                MM(S, ps[bi_][:], wg[:, 1, c, :], ub[c][:], True, True, wgids + [("ub", c)], [("ps", bi_)])
                ACT(S, rr[c][:], ps[br_][:], AF.Exp, [("ps", br_), "pvd"], [("rr", c)],
                    bias=pv[:, 18 + c:19 + c], scale=-1.0)
                ACT(S, ii[c][:], ps[bi_][:], AF.Exp, [("ps", bi_), "pvd"], [("ii", c)],
                    bias=pv[:, 20 + c:21 + c], scale=-1.0)
                TS(S, "vector", rr[c][:], rr[c][:], 1.0, None, ALU.add, None, [("rr", c)], [("rr", c)])
                RCP(S, rr[c][:], rr[c][:], [("rr", c)], [("rr", c)])
                TS(S, "gpsimd", ii[c][:], ii[c][:], 1.0, 1.0, ALU.add, ALU.mult, [("ii", c)], [("ii", c)])
                RCP(S, ii[c][:], ii[c][:], [("ii", c)], [("ii", c)])
                ACT(S, aa[c][:], rr[c][:], AF.Exp, [("rr", c), "pvs"], [("aa", c)], scale=pv[:, 22 + c:23 + c])
                STT(S, bb[c][:], aa[c][:], -1.0, aa[c][:], ALU.mult, ALU.mult, [("aa", c)], [("bb", c)])
                TS(S, "vector", bb[c][:], bb[c][:], 1.0, 1e-30, ALU.add, ALU.max, [("bb", c)], [("bb", c)])
                ACT(S, bb[c][:], bb[c][:], AF.Ln, [("bb", c)], [("bb", c)])
                ACT(S, bb[c][:], bb[c][:], AF.Exp, [("bb", c)], [("bb", c)], scale=0.5)
                TT(S, "gpsimd", ii[c][:], ii[c][:], u[c][:], ALU.mult, [("ii", c), ("u", c)], [("ii", c)])
                TT(S, "vector", bb[c][:], bb[c][:], ii[c][:], ALU.mult, [("bb", c), ("ii", c)], [("bb", c)])
                hcur = hs[c][tc % 2]
                if tc == 0:
                    init = 0.0
                    rds = [("aa", c), ("bb", c)]
                else:
                    init = hs[c][(tc - 1) % 2][:, 511:512]
                    rds = [("aa", c), ("bb", c), ("hs", c, (tc - 1) % 2)]
                SCAN(S, hcur[:], aa[c][:], bb[c][:], init, rds, [("hs", c, tc % 2)])
                TT(S, "gpsimd", gt[c][:], gl[c][:], gl[c][:], ALU.mult, [("gl", c)], [("gt", c)])
                TS(S, "gpsimd", gt[c][:], gt[c][:], 0.044715, 1.0, ALU.mult, ALU.add, [("gt", c)], [("gt", c)])
                TT(S, "gpsimd", gt[c][:], gt[c][:], gl[c][:], ALU.mult, [("gt", c), ("gl", c)], [("gt", c)])
                ACT(S, gu[c][:], gt[c][:], AF.Exp, [("gt", c)], [("gu", c)], scale=-1.5957691216057308)
                TS(S, "vector", gu[c][:], gu[c][:], 1.0, None, ALU.add, None, [("gu", c)], [("gu", c)])
                RCP(S, gu[c][:], gu[c][:], [("gu", c)], [("gu", c)])
                TT(S, "gpsimd", gu[c][:], gu[c][:], gl[c][:], ALU.mult, [("gu", c), ("gl", c)], [("gu", c)])
                TT(S, "vector", lo[c][:], gu[c][:], hcur[:], ALU.mult, [("gu", c), ("hs", c, tc % 2)],
                   [("lo", c)])
                DMA(S, "sync", mix_out[2 + c, :, t0:t0 + 512], lo[c][:], "lo%d" % c, [("lo", c)],
                    [("mixd", 2 + c, tc)])

        S.barrier()

        triinc = C.cb[:, 0, :]
        trilow = C.cb[:, 1, :]
        mask = C.cf[:, 2, :]
        NQC = NT // 512
        for cp in (range(2) if "attn" in _P1_PARTS else []):
            for qc in range(NQC):
                kmax = 4 * qc + 3
                nkb = kmax + 1
                obank = 2 + (qc % 2)
                q0 = qc * 512

                def cols_of(kb):
                    i_d = kb - 4 * qc
                    return (128 * i_d if i_d > 0 else 0), i_d

                def stage0(i):
                    kb = kmax - i
                    c0, i_d = cols_of(kb)
                    for h in range(2):
                        pr = slice(h * 64, (h + 1) * 64)
                        zb = 4 + 2 * h + (i % 2)
                        MM(S, ps[zb][:, c0:512], kT[pr, cp, kb * 128:(kb + 1) * 128],
                           qT[pr, cp, q0 + c0:q0 + 512], True, True, [], [("ps", zb)])
                    for h in range(2):
                        zb = 4 + 2 * h + (i % 2)
                        ACT(S, eb[h][i % 2][:, c0:512], ps[zb][:, c0:512], AF.Exp, [("ps", zb)],
                            [("e", h, i % 2)])
                    if i_d >= 0:
                        for h in range(2):
                            TT(S, "vector", eb[h][i % 2][:, c0:c0 + 128], eb[h][i % 2][:, c0:c0 + 128], mask,
                               ALU.mult, [("e", h, i % 2), "cst_f"], [("e", h, i % 2)])
                    for h in range(2):
                        ACT(S, spb[h][i % 2][:, c0:512], eb[h][i % 2][:, c0:512], AF.Ln,
                            [("e", h, i % 2), "kc0"], [("sp", h, i % 2)], bias=C.kc[:, 0:1])

                def stage1(i):
                    kb = kmax - i
                    c0, i_d = cols_of(kb)
                    first = (i == 0)
                    for h in range(2):
                        MM(S, ps[h][:, c0:512], triinc, spb[h][i % 2][:, c0:512], first, False,
                           ["cst_b", ("sp", h, i % 2)], [("acc", h)], skip=True)
                    for h in range(2):
                        ACT(S, gb[h][i % 2][:, c0:512], ps[h][:, c0:512], AF.Exp, [("acc", h)],
                            [("g", h, i % 2)], scale=-1.0)
                    for h in range(2):
                        if i + 1 < nkb:
                            MM(S, ps[h][:, c0:512], trilow, spb[h][i % 2][:, c0:512], False, False,
                               ["cst_b", ("sp", h, i % 2)], [("acc", h)], skip=True)
                    for h in range(2):
                        TT(S, "vector", wb[h][i % 2][:, c0:512], eb[h][i % 2][:, c0:512],
                           gb[h][i % 2][:, c0:512], ALU.mult, [("e", h, i % 2), ("g", h, i % 2)],
                           [("w", h, i % 2)])
                    for h in range(2):
                        pr = slice(h * 64, (h + 1) * 64)
                        MM(S, ps[obank][pr, c0:512], V[:, kb, cp * 128 + h * 64:cp * 128 + (h + 1) * 64],
                           wb[h][i % 2][:, c0:512], first, i == nkb - 1, [("w", h, i % 2)],
                           [("o", obank, h)], skip=True)

                for i in range(nkb + 1):
                    if i < nkb:
                        stage0(i)
                    if i >= 1:
                        stage1(i - 1)
                o = ost[qc % 2]
                CP(S, "vector", o[:], ps[obank][:], [("o", obank, 0), ("o", obank, 1)], [("ost", qc % 2)])
                DMA(S, "sync", mix_out[cp, :, q0:q0 + 512], o[:], "ost%d" % (qc % 2), [("ost", qc % 2)],
                    [("mixd", cp, qc)])
        S.barrier()
        S.flush()


def emit_P2(C, x_dram, mix_dram, w_out_dram, pv2_dram, w_up_dram, w_down_dram, x_out, hT_out, NT):
    nc, S = C.nc, C.S
    NTC = NT // 512
    NE = 8
    with ExitStack() as es2:
        sb = lambda n, sh, dt: es2.enter_context(nc.sbuf_tensor(_uname(n), sh, dt))
        ps = C.ps
        x_sb = sb("x_sb", [128, 8, NT], F32)
        h2 = sb("h2", [128, 8, NT], BF16)
        w_out = sb("w_out", [128, 8, 1024], BF16)
        pv2 = sb("pv2", [128, 16], F32)
        mixs = [sb("mixs%d" % i, [128, 8, 512], BF16) for i in range(2)]
        tmp = Ctx()
        tmp.sq = [sb("nsq%d" % i, [128, 512], F32) for i in range(3)]
        tmp.rstd = [sb("nrstd0", [128, 512], F32)]
        wu = [sb("wu%d" % i, [128, 8, 512], BF16) for i in range(2)]
        wd = [sb("wd%d" % i, [128, 4, 1024], BF16) for i in range(2)]
        rl = [sb("rl%d" % i, [128, 512], F32) for i in range(3)]
        hid = [sb("hid%d" % i, [128, 4, 512], BF16) for i in range(2)]

        xid = lambda tc, k: ("x", tc, k)
        DMA(S, "sync", pv2[:], pv2_dram, "pv2", [], ["pv2"])
        DMA(S, "gpsimd", w_out[:], w_out_dram.rearrange("(c p) n -> p c n", p=128), "w_out", [], ["w_out"])
        for tc in range(NTC):
            DMA(S, "sync", x_sb[:, :, tc * 512:(tc + 1) * 512],
                x_dram[:, :, tc * 512:(tc + 1) * 512].rearrange("c p t -> p c t"), "x%d" % (tc % 4),
                [], [xid(tc, k) for k in range(8)])

        def load_mix(tc):
            DMA(S, "sync", mixs[tc % 2][:], mix_dram[:, :, tc * 512:(tc + 1) * 512].rearrange("c p t -> p c t"),
                "mixs%d" % (tc % 2), [], [("mixs", tc % 2)])

        def load_w(e):
            s = e % 2
            DMA(S, "gpsimd", wu[s][:], w_up_dram[:, e * 512:(e + 1) * 512].rearrange("(c p) n -> p c n", p=128),
                "wu%d" % s, [], [("wu", s)])
            DMA(S, "gpsimd", wd[s][:], w_down_dram[e * 512:(e + 1) * 512, :].rearrange("(f p) n -> p f n", p=128),
                "wd%d" % s, [], [("wd", s)])

        load_mix(0)
        load_w(0)
        bc = [0]

        def nb(lo_, n):
            b = lo_ + bc[0] % n
            bc[0] += 1
            return b

        for tc in range(NTC):
            if tc + 1 < NTC:
                load_mix(tc + 1)
            for m in range(8):
                bank = nb(0, 4)
                for k in range(8):
                    MM(S, ps[bank][:], w_out[:, k, m * 128:(m + 1) * 128], mixs[tc % 2][:, k, :], k == 0, k == 7,
                       ["w_out", ("mixs", tc % 2)], [("ps", bank)])
                xs = x_sb[:, m, tc * 512:(tc + 1) * 512]
                TT(S, "vector", xs, ps[bank][:], xs, ALU.add, [("ps", bank), xid(tc, m)], [xid(tc, m)])
        for tc in range(NTC):
            emit_norm(C, x_sb, lambda k, tc=tc: xid(tc, k), tc * 512, pv2[:, 0:8], "pv2",
                      lambda m, tc=tc: h2[:, m, tc * 512:(tc + 1) * 512], lambda m, tc=tc: ("h2", tc, m), tmp, 4)
        for e in range(NE):
            if e + 1 < NE:
                load_w(e + 1)
            s = e % 2
            for tc in range(NTC):
                hd = hid[tc % 2]
                for f in range(4):
                    bank = nb(0, 4)
                    for k in range(8):
                        MM(S, ps[bank][:], wu[s][:, k, f * 128:(f + 1) * 128], h2[:, k, tc * 512:(tc + 1) * 512],
                           k == 0, k == 7, [("wu", s), ("h2", tc, k)], [("ps", bank)])
                    r = rl[(tc * 4 + f) % 3]
                    rid = ("rl", (tc * 4 + f) % 3)
                    TS(S, "vector", r[:], ps[bank][:], 0.0, None, ALU.max, None, [("ps", bank)], [rid])
                    TT(S, "gpsimd", hd[:, f, :], r[:], r[:], ALU.mult, [rid], [("hid", tc % 2, f)])
                for m in range(8):
                    bank = nb(4, 4)
                    for f in range(4):
                        MM(S, ps[bank][:], wd[s][:, f, m * 128:(m + 1) * 128], hd[:, f, :], f == 0, f == 3,
                           [("wd", s), ("hid", tc % 2, f)], [("ps", bank)])
                    xs = x_sb[:, m, tc * 512:(tc + 1) * 512]
                    TT(S, "vector", xs, ps[bank][:], xs, ALU.add, [("ps", bank), xid(tc, m)], [xid(tc, m)])
        for tc in range(NTC):
            DMA(S, "sync", x_out[:, :, tc * 512:(tc + 1) * 512].rearrange("c p t -> p c t"),
                x_sb[:, :, tc * 512:(tc + 1) * 512], "xo", [xid(tc, k) for k in range(8)], [("xo", tc)])
        if hT_out is not None:
            for tc in range(NTC):
                emit_norm(C, x_sb, lambda k, tc=tc: xid(tc, k), tc * 512, pv2[:, 8:16], "pv2",
                          lambda m, tc=tc: h2[:, m, tc * 512:(tc + 1) * 512], lambda m, tc=tc: ("h2", tc, m),
                          tmp, 4)
                DMA(S, "sync", hT_out[:, :, tc * 512:(tc + 1) * 512].rearrange("c p t -> p c t"),
                    h2[:, :, tc * 512:(tc + 1) * 512], "ho", [("h2", tc, m) for m in range(8)], [("ho", tc)])
        S.barrier()
        S.flush()


def build_P0(NT):
    nc = bass.Bass("TRN2", target_bir_lowering=False)
    x = nc.dram_tensor("x", [8, 128, NT], F32, kind="ExternalInput").ap()
    pv2 = nc.dram_tensor("pv2", [128, 16], F32, kind="ExternalInput").ap()
    cst = nc.dram_tensor("cst", [128, 5, 128], F32, kind="ExternalInput").ap()
    hT = nc.dram_tensor("hT", [8, 128, NT], BF16, kind="ExternalOutput").ap()
    with ExitStack() as es:
        S = Sched(nc, es)
        C = setup_common(nc, es, S, cst)
        emit_P0(C, x, pv2, hT, NT)
    return nc


def build_P1():
    nc = bass.Bass("TRN2", target_bir_lowering=False)
    hT = nc.dram_tensor("hT", [8, 128, S_LEN], BF16, kind="ExternalInput").ap()
    w_in = nc.dram_tensor("w_in", [D, 1280], F32, kind="ExternalInput").ap()
    pv = nc.dram_tensor("pv", [128, 18], F32, kind="ExternalInput").ap()
    wg = nc.dram_tensor("wg", [2, 4, 64, 64], F32, kind="ExternalInput").ap()
    cst = nc.dram_tensor("cst", [128, 5, 128], F32, kind="ExternalInput").ap()
    mix = nc.dram_tensor("mix", [4, 128, S_LEN], BF16, kind="ExternalOutput").ap()
    with ExitStack() as es:
        S = Sched(nc, es)
        C = setup_common(nc, es, S, cst)
        emit_P1(C, hT, w_in, pv, wg, mix)
    return nc


def build_P2(NT, with_h):
    nc = bass.Bass("TRN2", target_bir_lowering=False)
    x = nc.dram_tensor("x", [8, 128, NT], F32, kind="ExternalInput").ap()
    mix = nc.dram_tensor("mix", [8, 128, NT], BF16, kind="ExternalInput").ap()
    w_out = nc.dram_tensor("w_out", [D, D], F32, kind="ExternalInput").ap()
    pv2 = nc.dram_tensor("pv2", [128, 16], F32, kind="ExternalInput").ap()
    w_up = nc.dram_tensor("w_up", [D, 4 * D], F32, kind="ExternalInput").ap()
    w_down = nc.dram_tensor("w_down", [4 * D, D], F32, kind="ExternalInput").ap()
    cst = nc.dram_tensor("cst", [128, 5, 128], F32, kind="ExternalInput").ap()
    xo = nc.dram_tensor("xo", [8, 128, NT], F32, kind="ExternalOutput").ap()
    hT = nc.dram_tensor("hT", [8, 128, NT], BF16, kind="ExternalOutput").ap() if with_h else None
    with ExitStack() as es:
        S = Sched(nc, es)
        C = setup_common(nc, es, S, cst)
        emit_P2(C, x, mix, w_out, pv2, w_up, w_down, xo, hT, NT)
    return nc


def make_consts():
    j = np.arange(128)[:, None]
    s = np.arange(128)[None, :]
    cst = np.zeros((128, 5, 128), np.float32)
    cst[:, 0, :] = (j >= s)
    cst[:, 1, :] = (j < s)
    cst[:, 2, :] = (s > j)
    cst[:, 3, :] = 1.0
    cst[:, 4, :] = ((j // 64) == (s // 64))
    return cst


def pcol(v):
    v = np.asarray(v, np.float32)
    return np.ascontiguousarray(v.reshape(-1, 128).T)


def make_pv(l, hh, inp):
    pv = np.zeros((128, 18), np.float32)
    sl = slice(hh * 256, (hh + 1) * 256)
    cw = np.asarray(inp["conv_w"][l])[:, sl]
    for c in range(2):
        for k in range(4):
            pv[:, c * 4 + k] = cw[k, c * 128:(c + 1) * 128]
    pv[:, 8:10] = pcol(np.asarray(inp["conv_b"][l])[sl])
    pv[:, 10:12] = pcol(np.asarray(inp["b_rg"][l])[sl])
    pv[:, 12:14] = pcol(np.asarray(inp["b_ig"][l])[sl])
    pv[:, 14:16] = pcol(np.asarray(inp["lru_lambda"][l])[sl])
    pv[:, 16] = np.tile(np.asarray(inp["q_norm_g"][l]), 2)
    pv[:, 17] = np.tile(np.asarray(inp["k_norm_g"][l]), 2)
    return pv


def make_w_in(l, hh, inp):
    w = np.asarray(inp["w_in"][l])
    cols = []
    for base in (0, 512, 1024, 1536, 2048):
        cols.append(w[:, base + hh * 256: base + (hh + 1) * 256])
    return np.ascontiguousarray(np.concatenate(cols, axis=1))


def make_wg(l, hh, inp):
    wr = np.asarray(inp["w_rg"][l])[hh * 4:(hh + 1) * 4]
    wi = np.asarray(inp["w_ig"][l])[hh * 4:(hh + 1) * 4]
    return np.ascontiguousarray(np.stack([wr, wi], axis=0))


def make_w_out(l, inp):
    w = np.asarray(inp["w_out"][l])
    rows = []
    for hh in range(2):
        rows.append(w[hh * 256:(hh + 1) * 256])
        rows.append(w[512 + hh * 256:512 + (hh + 1) * 256])
    return np.ascontiguousarray(np.concatenate(rows, axis=0))


def make_pv2(g2, g1n):
    pv2 = np.zeros((128, 16), np.float32)
    if g2 is not None:
        pv2[:, 0:8] = pcol(g2)
    if g1n is not None:
        pv2[:, 8:16] = pcol(g1n)
    return pv2


_PROGS = {}
_DBG = None
_P1_PARTS = ("proj", "lru", "attn")


def _prog(key, fn):
    if key not in _PROGS:
        _PROGS[key] = fn()
    return _PROGS[key]


def kernel(**inp):
    x = np.asarray(inp["x"], np.float32)
    cst = make_consts()
    NTH = S_LEN // 2
    cores = list(range(8))
    xT = [np.ascontiguousarray(x[b].T).reshape(8, 128, S_LEN) for b in range(NB)]
    x_own = [np.ascontiguousarray(xT[c // 2][:, :, (c % 2) * NTH:(c % 2 + 1) * NTH]) for c in cores]

    nc0 = build_P0(NTH)
    pv2 = make_pv2(None, inp["norm1_g"][0])
    res = run_bass_kernel_spmd(nc0, [{"x": x_own[c], "pv2": pv2, "cst": cst} for c in cores], core_ids=cores)
    h_own = [np.asarray(res.results[c]["hT"]) for c in cores]
    if _DBG is not None:
        _DBG["h0"] = h_own

    for l in range(DEPTH):
        hT_b = [np.ascontiguousarray(np.concatenate([h_own[2 * b], h_own[2 * b + 1]], axis=2)) for b in range(NB)]
        nc1 = build_P1()
        maps = []
        for c in cores:
            b, hh = c // 2, c % 2
            maps.append({"hT": hT_b[b], "w_in": make_w_in(l, hh, inp), "pv": make_pv(l, hh, inp),
                         "wg": make_wg(l, hh, inp), "cst": cst})
        res = run_bass_kernel_spmd(nc1, maps, core_ids=cores)
        mix_c = [np.asarray(res.results[c]["mix"]) for c in cores]
        if _DBG is not None:
            _DBG["mix%d" % l] = mix_c
            if _DBG.get("stop") == "mix%d" % l:
                return None
        last = (l == DEPTH - 1)
        nc2 = build_P2(NTH, not last)
        w_out = make_w_out(l, inp)
        pv2 = make_pv2(inp["norm2_g"][l], None if last else inp["norm1_g"][l + 1])
        w_up = np.ascontiguousarray(np.asarray(inp["w_up"][l], np.float32))
        w_down = np.ascontiguousarray(np.asarray(inp["w_down"][l], np.float32))
        maps = []
        for c in cores:
            b, th = c // 2, c % 2
            tsl = slice(th * NTH, (th + 1) * NTH)
            mix = np.ascontiguousarray(np.concatenate([mix_c[2 * b][:, :, tsl], mix_c[2 * b + 1][:, :, tsl]], axis=0))
            maps.append({"x": x_own[c], "mix": mix, "w_out": w_out, "pv2": pv2, "w_up": w_up,
                         "w_down": w_down, "cst": cst})
        res = run_bass_kernel_spmd(nc2, maps, core_ids=cores)
        x_own = [np.asarray(res.results[c]["xo"]) for c in cores]
        if _DBG is not None:
            _DBG["x%d" % l] = x_own
            if _DBG.get("stop") == "x%d" % l:
                return None
        if not last:
            h_own = [np.asarray(res.results[c]["hT"]) for c in cores]

    out = np.zeros((NB, S_LEN, D), np.float32)
    for c in cores:
        b, th = c // 2, c % 2
        out[b, th * NTH:(th + 1) * NTH, :] = x_own[c].reshape(D, NTH).T
    return out
```
